# Optimizing a Trainium2 kernel written in Bass

```python
import math
import jax, jax.numpy as jnp
from jax import lax
import numpy as np

D_MODEL = 2048
BATCH = 16
SEQ = 2048
DEPTH = 2
DEC_BATCH = 4
DEC_SEQ = 4096
PAST_LEN = 128

HEAD_DIM = 128
N_Q_HEADS = 8
N_KV_HEADS = 2
GROUP = N_Q_HEADS // N_KV_HEADS
WINDOW = 128
BLOCK = 128
ROPE_THETA = 10000.0
Q_WIDTH = N_Q_HEADS * HEAD_DIM
KV_WIDTH = N_KV_HEADS * HEAD_DIM

POOL_WINDOWS = (2, 4, 8, 16)
N_POOL_GROUPS = len(POOL_WINDOWS)
POOL_WIDTH = D_MODEL // 2
POOL_GROUP_DIM = POOL_WIDTH // N_POOL_GROUPS

Q_OFF = 0
K_OFF = Q_OFF + Q_WIDTH
V_OFF = K_OFF + KV_WIDTH
U_OFF = V_OFF + KV_WIDTH
G_OFF = U_OFF + POOL_WIDTH
IN_WIDTH = G_OFF + 2 * D_MODEL

D_FF = ((8 * D_MODEL // 3 + 255) // 256) * 256

ALPHA = (2.0 * DEPTH) ** 0.25
BETA = (8.0 * DEPTH) ** -0.25
LN_EPS = 1e-5
NEG_INF = -1e30

kernel_name = "hybrid_winattn_pool_deepnorm_encoder"


def _layer_norm(x, g, b):
    xf = x.astype(jnp.float32)
    mu = jnp.mean(xf, axis=-1, keepdims=True)
    var = jnp.mean(jnp.square(xf - mu), axis=-1, keepdims=True)
    y = (xf - mu) * lax.rsqrt(var + LN_EPS)
    return (y * g.astype(jnp.float32) + b.astype(jnp.float32)).astype(x.dtype)


def _rope(t, pos):
    half = HEAD_DIM // 2
    inv_freq = ROPE_THETA ** (-jnp.arange(half, dtype=jnp.float32) / half)
    ang = pos.astype(jnp.float32)[:, None] * inv_freq[None, :]
    cos = jnp.cos(ang)[None, :, None, :]
    sin = jnp.sin(ang)[None, :, None, :]
    tf = t.astype(jnp.float32)
    t1, t2 = tf[..., :half], tf[..., half:]
    out = jnp.concatenate([t1 * cos - t2 * sin, t2 * cos + t1 * sin], axis=-1)
    return out.astype(t.dtype)


def _band_blocks(t, nb):
    B = t.shape[0]
    tp = jnp.pad(t, ((0, 0), (BLOCK, BLOCK), (0, 0), (0, 0)))
    tp = tp.reshape(B, nb + 2, BLOCK, N_KV_HEADS, HEAD_DIM)
    return jnp.concatenate([tp[:, :-2], tp[:, 1:-1], tp[:, 2:]], axis=2)


def _window_attention(q, k, v, sink):
    B, S = q.shape[0], q.shape[1]
    nb = S // BLOCK
    qb = q.reshape(B, nb, BLOCK, N_KV_HEADS, GROUP, HEAD_DIM)
    kb = _band_blocks(k, nb)
    vb = _band_blocks(v, nb)
    scale = HEAD_DIM ** -0.5
    s = jnp.einsum('bnqhgd,bnkhd->bnhgqk', qb, kb).astype(jnp.float32) * scale
    qi = jnp.arange(BLOCK)
    kj = jnp.arange(3 * BLOCK)
    rel = kj[None, :] - qi[:, None]
    band_ok = (rel >= BLOCK - WINDOW) & (rel <= BLOCK + WINDOW)
    j_abs = jnp.arange(nb)[:, None] * BLOCK - BLOCK + kj[None, :]
    in_range = (j_abs >= 0) & (j_abs < S)
    mask = band_ok[None, :, :] & in_range[:, None, :]
    s = jnp.where(mask[None, :, None, None, :, :], s, NEG_INF)
    sk = sink.astype(jnp.float32).reshape(N_KV_HEADS, GROUP)[None, None, :, :, None, None]
    m = jnp.maximum(jnp.max(s, axis=-1, keepdims=True), sk)
    p = jnp.exp(s - m)
    denom = jnp.sum(p, axis=-1, keepdims=True) + jnp.exp(sk - m)
    o = jnp.einsum('bnhgqk,bnkhd->bnhgqd', p, vb.astype(jnp.float32)) / denom
    o = o.transpose(0, 1, 4, 2, 3, 5).reshape(B, S, Q_WIDTH)
    return o.astype(q.dtype)


def _multiscale_pool(u, w_mix, scale):
    B, S, _ = u.shape
    uf = u.astype(jnp.float32).reshape(B, S, N_POOL_GROUPS, POOL_GROUP_DIM)
    c = jnp.pad(lax.cumsum(uf, axis=1), ((0, 0), (1, 0), (0, 0), (0, 0)))
    pos = jnp.arange(S)
    pooled = []
    for gi, w in enumerate(POOL_WINDOWS):
        lo = jnp.clip(pos - w // 2, 0, S)
        hi = jnp.clip(pos + (w - w // 2), 0, S)
        cg = c[:, :, gi]
        win_sum = jnp.take(cg, hi, axis=1) - jnp.take(cg, lo, axis=1)
        pooled.append(win_sum / (hi - lo).astype(jnp.float32)[None, :, None])
    pooled = jnp.stack(pooled, axis=2) - uf
    mixed = jnp.einsum('bsgc,gcd->bsgd', pooled.astype(u.dtype), w_mix)
    return mixed.reshape(B, S, POOL_WIDTH) * scale


def _layer(x, w_in, sink, w_attn_proj, w_pool_mix, pool_scale, w_pool_proj, w_out,
           ln1_g, ln1_b, w_ffn_in, w_ffn_out, ln2_g, ln2_b):
    B, S, _ = x.shape
    h = x @ w_in
    q = h[..., Q_OFF:K_OFF].reshape(B, S, N_Q_HEADS, HEAD_DIM)
    k = h[..., K_OFF:V_OFF].reshape(B, S, N_KV_HEADS, HEAD_DIM)
    v = h[..., V_OFF:U_OFF].reshape(B, S, N_KV_HEADS, HEAD_DIM)
    u = h[..., U_OFF:G_OFF]
    g_a = h[..., G_OFF:G_OFF + D_MODEL]
    g_b = h[..., G_OFF + D_MODEL:]
    pos = jnp.arange(S)
    q = _rope(q, pos)
    k = _rope(k, pos)
    a = _window_attention(q, k, v, sink) @ w_attn_proj
    b = _multiscale_pool(u, w_pool_mix, pool_scale) @ w_pool_proj
    mix = (jax.nn.sigmoid(g_a) * a + jax.nn.sigmoid(g_b) * b) @ w_out
    x = _layer_norm(ALPHA * x + mix, ln1_g, ln1_b)
    f = x @ w_ffn_in
    f = (jax.nn.silu(f[..., :D_FF]) * f[..., D_FF:]) @ w_ffn_out
    x = _layer_norm(ALPHA * x + f, ln2_g, ln2_b)
    return x


def _trunk(x, w_in, sink, w_attn_proj, w_pool_mix, pool_scale, w_pool_proj, w_out,
           ln1_g, ln1_b, w_ffn_in, w_ffn_out, ln2_g, ln2_b):
    for l in range(DEPTH):
        x = _layer(x, w_in[l], sink[l], w_attn_proj[l], w_pool_mix[l], pool_scale[l],
                   w_pool_proj[l], w_out[l], ln1_g[l], ln1_b[l], w_ffn_in[l],
                   w_ffn_out[l], ln2_g[l], ln2_b[l])
    return x


def setup_inputs(seed: int = 0) -> dict:
    key = jax.random.key(seed)
    ks = jax.random.split(key, 16)
    f32 = jnp.float32
    nrm = lambda k, shape: jax.random.normal(k, shape, dtype=f32)
    x_prompt = nrm(ks[0], (BATCH, SEQ, D_MODEL))
    x_sample = nrm(ks[1], (DEC_BATCH, DEC_SEQ, D_MODEL))
    w_in = nrm(ks[2], (DEPTH, D_MODEL, IN_WIDTH)) * D_MODEL ** -0.5
    w_in = w_in.at[:, :, V_OFF:U_OFF].multiply(BETA)
    sink = nrm(ks[3], (DEPTH, N_Q_HEADS)) * 0.5
    w_attn_proj = nrm(ks[4], (DEPTH, Q_WIDTH, D_MODEL)) * Q_WIDTH ** -0.5
    w_pool_mix = nrm(ks[5], (DEPTH, N_POOL_GROUPS, POOL_GROUP_DIM, POOL_GROUP_DIM)) * POOL_GROUP_DIM ** -0.5
    pool_scale = 1.0 + 0.02 * nrm(ks[6], (DEPTH, POOL_WIDTH))
    w_pool_proj = nrm(ks[7], (DEPTH, POOL_WIDTH, D_MODEL)) * POOL_WIDTH ** -0.5
    w_out = nrm(ks[8], (DEPTH, D_MODEL, D_MODEL)) * (BETA * D_MODEL ** -0.5)
    ln1_g = 1.0 + 0.02 * nrm(ks[9], (DEPTH, D_MODEL))
    ln1_b = 0.02 * nrm(ks[10], (DEPTH, D_MODEL))
    w_ffn_in = nrm(ks[11], (DEPTH, D_MODEL, 2 * D_FF)) * D_MODEL ** -0.5
    w_ffn_out = nrm(ks[12], (DEPTH, D_FF, D_MODEL)) * (BETA * D_FF ** -0.5)
    ln2_g = 1.0 + 0.02 * nrm(ks[13], (DEPTH, D_MODEL))
    ln2_b = 0.02 * nrm(ks[14], (DEPTH, D_MODEL))
    return {"x_prompt": x_prompt, "x_sample": x_sample, "w_in": w_in, "sink": sink,
            "w_attn_proj": w_attn_proj, "w_pool_mix": w_pool_mix, "pool_scale": pool_scale,
            "w_pool_proj": w_pool_proj, "w_out": w_out, "ln1_g": ln1_g, "ln1_b": ln1_b,
            "w_ffn_in": w_ffn_in, "w_ffn_out": w_ffn_out, "ln2_g": ln2_g, "ln2_b": ln2_b}


def reference(x_prompt, x_sample, w_in, sink, w_attn_proj, w_pool_mix, pool_scale,
              w_pool_proj, w_out, ln1_g, ln1_b, w_ffn_in, w_ffn_out, ln2_g, ln2_b):
    y_prompt = _trunk(x_prompt, w_in, sink, w_attn_proj, w_pool_mix, pool_scale,
                      w_pool_proj, w_out, ln1_g, ln1_b, w_ffn_in, w_ffn_out, ln2_g, ln2_b)
    y_sample = _trunk(x_sample, w_in, sink, w_attn_proj, w_pool_mix, pool_scale,
                      w_pool_proj, w_out, ln1_g, ln1_b, w_ffn_in, w_ffn_out, ln2_g, ln2_b)
    return (y_prompt, y_sample)
```

```python
import math
from contextlib import ExitStack
import numpy as np
import concourse.bass as bass
import concourse.mybir as mybir
from concourse.bass_utils import run_bass_kernel_spmd

F32 = mybir.dt.float32
F32R = mybir.dt.float32r
AF = mybir.ActivationFunctionType
ALU = mybir.AluOpType

D = 2048
KC = 16
INW = 6656
DFF = 5632
NJ = 44
DEPTH = 2
ALPHA = (2.0 * DEPTH) ** 0.25
EPS = 1e-5
SCALE = 128 ** -0.5
NEG = -1e30
K_OFF, V_OFF, U_OFF, G_OFF = 1024, 1280, 1536, 2560
POOLW = (2, 4, 8, 16)
T = 512

FULL_SLOTS = [
    dict(name="A", in_off=0, nblk=16, l1=[0, 4, 8, 12], l2=[0, 4, 8, 12], out_off=0, out_b0=0, tbl_off=0, ctype=False),
    dict(name="B", in_off=2048, nblk=16, l1=[0, 4, 8, 12], l2=[0, 4, 8, 12], out_off=2048, out_b0=0, tbl_off=0, ctype=False),
    dict(name="C", in_off=4096, nblk=20, l1=[0, 4, 8, 12, 16], l2=[2, 6, 10, 14], out_off=4096, out_b0=2, tbl_off=2048, ctype=True),
]
FULL_CFG = dict(slots=FULL_SLOTS, ntok_in=6656, ntok_out=6144, ntbl=4608)


class Res:
    __slots__ = ("name", "lw", "rd", "al")

    def __init__(self, name):
        self.name = name
        self.lw = None
        self.rd = {}
        self.al = []


class Op:
    __slots__ = ("eng", "fn", "deps", "signal", "sem", "val", "is_dma", "ndma")

    def __init__(self, eng, fn):
        self.eng = eng
        self.fn = fn
        self.deps = []
        self.signal = False
        self.sem = None
        self.val = 0
        self.is_dma = False
        self.ndma = 0


class Plan:
    ENGS = ("pe", "act", "dve", "pool", "sp")
    SYNC_SAME = ("act", "dve", "pool")

    def __init__(self):
        self.q = {e: [] for e in self.ENGS}
        self.res = {}
        self.dma_cnt = {}

    def r(self, name):
        x = self.res.get(name)
        if x is None:
            x = self.res[name] = Res(name)
        return x

    def alias(self, a, b):
        ra, rb = self.r(a), self.r(b)
        ra.al.append(rb)
        rb.al.append(ra)

    def _add(self, op, reads, writes):
        deps = []
        for n in reads:
            x = self.r(n)
            if x.lw is not None:
                deps.append(x.lw)
            for y in x.al:
                if y.lw is not None:
                    deps.append(y.lw)
        for n in writes:
            x = self.r(n)
            for y in [x] + x.al:
                if y.lw is not None:
                    deps.append(y.lw)
                deps.extend(y.rd.values())
        for d in deps:
            if d is op:
                continue
            if d.is_dma or d.eng != op.eng or (op.eng in self.SYNC_SAME and not op.is_dma):
                d.signal = True
                op.deps.append(d)
        for n in reads:
            self.r(n).rd[op.eng if not op.is_dma else ("dma", op.sem)] = op
        for n in writes:
            x = self.r(n)
            x.lw = op
            x.rd = {}
        self.q[op.eng].append(op)
        return op

    def op(self, eng, fn, reads=(), writes=()):
        return self._add(Op(eng, fn), reads, writes)

    def dma(self, eng, semkey, fn, ndma, reads=(), writes=()):
        o = Op(eng, fn)
        o.is_dma = True
        o.ndma = ndma
        o.sem = semkey
        self.dma_cnt[semkey] = self.dma_cnt.get(semkey, 0) + 16 * ndma
        o.val = self.dma_cnt[semkey]
        o.signal = True
        return self._add(o, reads, writes)

    def finalize(self):
        for e in self.ENGS:
            c = 0
            for o in self.q[e]:
                if not o.is_dma and o.signal:
                    c += 1
                    o.sem = "eng_" + e
                    o.val = c

    def sem_names(self):
        s = set(self.dma_cnt.keys())
        for e in self.ENGS:
            s.add("eng_" + e)
        return sorted(s)

    def emit(self, eng, e, sems):
        waited = {}
        for o in self.q[eng]:
            need = {}
            for d in o.deps:
                if need.get(d.sem, 0) < d.val:
                    need[d.sem] = d.val
            for s, v in need.items():
                if waited.get(s, 0) < v:
                    e.wait_ge(sems[s], v)
                    waited[s] = v
            if o.is_dma:
                o.fn(e, sems[o.sem])
            else:
                ins = o.fn(e)
                if o.signal:
                    ins.then_inc(sems[o.sem], 1)


def build_program(cfg):
    slots = cfg["slots"]
    NTI, NTO, NTB = cfg["ntok_in"], cfg["ntok_out"], cfg["ntbl"]
    nc = bass.Bass("TRN2", target_bir_lowering=False)

    def din(name, shape):
        return nc.dram_tensor(name, list(shape), F32, kind="ExternalInput").ap()

    xin = din("xin", [NTI, D])
    w_in = din("w_in", [DEPTH, D, INW])
    w_ap = din("w_attn_proj", [DEPTH, 1024, D])
    w_pm = din("w_pool_mix", [DEPTH, 4, 256, 256])
    w_pp = din("w_pool_proj", [DEPTH, 1024, D])
    w_out = din("w_out", [DEPTH, D, D])
    w_f1 = din("w_ffn_in", [DEPTH, D, 2 * DFF])
    w_f2 = din("w_ffn_out", [DEPTH, DFF, D])
    prm_d = din("prm", [128, 128])
    prm2_d = din("prm2", [128, 40])
    cos_d = din("cosT", [128, NTB])
    sin_d = din("sinT", [128, NTB])
    rc_d = din("rcT", [128, 4, NTB])
    cst_d = din("cst", [128, 1280])
    yout = nc.dram_tensor("yout", [NTO, D], F32, kind="ExternalOutput").ap()
    x1T = nc.dram_tensor("x1T", [KC, 128, NTI], F32, kind="Internal").ap()

    NWS = 4
    WS = [nc.alloc_sbuf_tensor(f"ws{i}", [128, 16, 128], F32R) for i in range(NWS)]
    XC = nc.alloc_sbuf_tensor("xc", [128, KC, T], F32R)
    R2 = nc.alloc_sbuf_tensor("r2", [128, 22528], F32R)
    SQ = [nc.alloc_sbuf_tensor(f"sq{i}", [128, T], F32R) for i in range(2)]
    COS = nc.alloc_sbuf_tensor("cos", [128, 768], F32)
    SIN = nc.alloc_sbuf_tensor("sin", [128, 768], F32)
    RC = nc.alloc_sbuf_tensor("rc", [128, 4, T], F32)
    NTMP = 3
    TMP = [nc.alloc_sbuf_tensor(f"tmp{i}", [128, 528], F32) for i in range(NTMP)]
    NPT = 2
    PT = [nc.alloc_sbuf_tensor(f"pt{i}", [128, T], F32R) for i in range(NPT)]
    LNT = [nc.alloc_sbuf_tensor(f"lnt{i}", [128, T], F32) for i in range(2)]
    STGS = [nc.alloc_sbuf_tensor(f"stg{i}", [128, D], F32) for i in range(2)]
    IDN = nc.alloc_sbuf_tensor("idn", [128, 128], F32)
    CSTR = nc.alloc_sbuf_tensor("cstr", [128, 1280], F32R)
    PRM = nc.alloc_sbuf_tensor("prm_s", [128, 128], F32)
    PRM2 = nc.alloc_sbuf_tensor("prm2_s", [128, 40], F32)
    ESINK = nc.alloc_sbuf_tensor("esink", [128, 8], F32)
    BANK = [nc.alloc_psum_tensor(f"bank{i}", [128, T], F32) for i in range(8)]

    IDR = CSTR[:, 0:128]
    EPSC = PRM2[:, 36:37]
    ONESR = CSTR[:, 128:256]

    def MB(which):
        return CSTR[:, 256 + which * 512: 256 + (which + 1) * 512]

    def r2(off, n, dt=F32R):
        a = R2[:, off:off + n]
        return a.bitcast(dt) if dt is not F32R else a

    def r2_3d(off, nch, w, a, b, dt=F32R):
        v = R2[:, off:off + nch * w].rearrange("p (c t) -> p c t", t=w)[:, :, a:b]
        return v.bitcast(dt) if dt is not F32R else v

    Q_O, K_O, VT_O, V_O, U_O, MX_O, XH_O = 0, 4096, 5632, 7168, 8704, 12928, 17024

    def qT(h, a=0, b=T):
        return r2(Q_O + h * T + a, b - a)

    def kT(kv, a, b):
        return r2(K_O + kv * 768 + a, b - a)

    def vT(kv, a, b, dt=F32R):
        return r2(VT_O + kv * 768 + a, b - a, dt)

    def Vtok(wb, kv):
        return r2(V_O + wb * 256 + kv * 128, 128)

    def uT(j, a, b, dt=F32R):
        return r2(U_O + j * 528 + a, b - a, dt)

    def mxT(j, a=0, b=T):
        return r2(MX_O + j * T + a, b - a)

    def XH(k, a, b):
        return r2(XH_O + k * 256 + a, b - a)

    def oT(k, a=0, b=T):
        return r2(XH_O + k * T + a, b - a)

    def mixT(m, dt=F32R, a=0, b=T):
        return r2(m * T + a, b - a, dt)

    def gT(j, dt=F32R, a=0, b=T):
        return r2(j * T + a, b - a, dt)

    P = Plan()
    XCALL = [f"xc{k}" for k in range(KC)]
    for m_ in range(16):
        for a in ("q", "k", "vT", "V"):
            P.alias(f"mix{m_}", a)
    for j_ in range(NJ):
        for a in ("q", "k", "vT", "V", "u", "mx", "xh", "o"):
            P.alias(f"g{j_}", a)
        if j_ < 16:
            P.alias(f"g{j_}", f"mix{j_}")
    P.alias("xh", "o")

    st = dict(tmp=0, pt=0, rot=0, wslot=0, acc=0, evac=0, stg=0)

    def stage():
        i = st["stg"] % 2
        st["stg"] += 1
        return STGS[i], f"stg{i}"

    def tmp():
        i = st["tmp"] % NTMP
        st["tmp"] += 1
        return TMP[i], f"tmp{i}"

    def ptile():
        i = st["pt"] % 4
        st["pt"] += 1
        return (PT[i], f"pt{i}") if i < 2 else (SQ[i - 2], f"sq{i - 2}")

    def bank(pool=(0, 1, 2, 3, 4, 5)):
        i = pool[st["rot"] % len(pool)]
        st["rot"] += 1
        return BANK[i], f"bank{i}"

    def wload(src_ap, nk, reads=()):
        i = st["wslot"] % NWS
        st["wslot"] += 1
        slot = WS[i]

        def fn(e, sem, slot=slot, src_ap=src_ap, nk=nk):
            e.dma_start(out=slot[:, 0:nk, :], in_=src_ap.rearrange("(k p) c -> p k c", p=128)).then_inc(sem, 16)
        P.dma("pool", f"w{i}", fn, 1, reads=list(reads), writes=[f"ws{i}"])
        return slot, f"ws{i}"

    def mm_group(out_ap, bname, pairs, reads, start=True, stop=True, extra_w=()):
        def fn(e, out_ap=out_ap, pairs=pairs, start=start, stop=stop):
            ins = None
            n = len(pairs)
            for i, (l, r) in enumerate(pairs):
                ins = e.matmul(out_ap, lhsT=l, rhs=r, start=(start and i == 0), stop=(stop and i == n - 1))
            return ins
        return P.op("pe", fn, reads=list(reads), writes=[bname] + list(extra_w))

    def c_fn(e, sem):
        e.dma_start(out=IDN[:], in_=cst_d[:, 0:128]).then_inc(sem, 16)
        e.dma_start(out=PRM[:], in_=prm_d).then_inc(sem, 16)
        e.dma_start(out=PRM2[:], in_=prm2_d).then_inc(sem, 16)
    P.dma("sp", "cst", c_fn, 3, writes=["idn", "prm", "prm2"])

    def c2_fn(e, sem):
        e.dma_start(out=CSTR[:], in_=cst_d).then_inc(sem, 16)
    P.dma("pool", "cst2", c2_fn, 1, writes=["cstr"])

    def evac_eng():
        st["evac"] += 1
        return "act" if st["evac"] % 2 else "dve"

    def copy_op(eng, out_ap, in_ap, reads, writes):
        if eng == "act":
            return P.op("act", lambda e, o=out_ap, i=in_ap: e.activation(out=o, in_=i, func=AF.Copy), reads=reads, writes=writes)
        return P.op("dve", lambda e, o=out_ap, i=in_ap: e.tensor_copy(out=o, in_=i), reads=reads, writes=writes)

    def layer_setup(l):
        P.op("act", lambda e: e.activation(out=ESINK[:], in_=PRM2[:, 16 + l * 8:16 + l * 8 + 8], func=AF.Exp),
             reads=["prm2"], writes=["esink"])

    pre = {"tbl": None, "blk": {}}

    def prep_tables(l, sl, s):
        nblk = sl["nblk"]
        toff = sl["tbl_off"]
        lo_b, hi_b = max(s - 1, 0), min(s + 5, nblk)

        def tfn(e, sem):
            c0 = (lo_b - (s - 1)) * 128
            n = (hi_b - lo_b) * 128
            e.dma_start(out=COS[:, c0:c0 + n], in_=cos_d[:, toff + lo_b * 128: toff + hi_b * 128]).then_inc(sem, 16)
            e.dma_start(out=SIN[:, c0:c0 + n], in_=sin_d[:, toff + lo_b * 128: toff + hi_b * 128]).then_inc(sem, 16)
            e.dma_start(out=RC[:], in_=rc_d[:, :, toff + s * 128: toff + s * 128 + T]).then_inc(sem, 16)
        P.dma("sp", "tbl", tfn, 3, writes=["cos", "sin", "rc"])

    def block_dma(sl, s, wb):
        row0 = sl["in_off"] + (s - 1 + wb) * 128
        STG, sgn = stage()

        def ifn(e, sem, row0=row0, STG=STG):
            e.dma_start(out=STG[:], in_=xin[row0:row0 + 128, :]).then_inc(sem, 16)
        P.dma("sp", sgn, ifn, 1, writes=[sgn])
        return STG, sgn

    def prefetch(l, sl, s):
        prep_tables(l, sl, s)
        pre["tbl"] = (l, sl["name"], s)
        if l == 0:
            for wb in (1, 2):
                pre["blk"][(l, sl["name"], s, wb)] = block_dma(sl, s, wb)

    def group(l, sl, s, next_unit=None):
        nblk = sl["nblk"]
        lv = s >= 1
        rv = s + 4 < nblk
        wbs = [wb for wb in range(6) if 0 <= s - 1 + wb < nblk]
        toff = sl["tbl_off"]
        name = sl["name"]

        cl, ch = 0, T
        if l == 0 and sl["ctype"]:
            if s == 0:
                cl = 128
            elif s + 4 >= nblk:
                ch = 384
        key = (l, name, s)
        if pre["tbl"] != key:
            prep_tables(l, sl, s)

        if not lv:
            P.op("dve", lambda e: e.memset(r2_3d(XH_O, KC, 256, 0, 128, F32), 0.0), writes=["xh"])
        if not rv:
            P.op("dve", lambda e: e.memset(r2_3d(XH_O, KC, 256, 128, 256, F32), 0.0), writes=["xh"])

        def load_block(wb):
                if (l, name, s, wb) in pre["blk"]:
                    STG, sgn = pre["blk"].pop((l, name, s, wb))
                else:
                    STG, sgn = block_dma(sl, s, wb)
                for c0 in range(0, KC, 4):
                    bk, bn = bank()

                    def tfn2(e, bk=bk, c0=c0, STG=STG):
                        ins = None
                        for c in range(4):
                            ins = e.transpose(bk[:, c * 128:(c + 1) * 128], STG[:, (c0 + c) * 128:(c0 + c + 1) * 128], IDN[:])
                        return ins
                    P.op("pe", tfn2, reads=[sgn, "idn"], writes=[bn])
                    src = bk[:, :].rearrange("p (c t) -> p c t", t=128)
                    if 1 <= wb <= 4:
                        dst = XC[:, c0:c0 + 4, (wb - 1) * 128: wb * 128]
                        wn = None
                    else:
                        a = 0 if wb == 0 else 128
                        dst = R2[:, XH_O + c0 * 256: XH_O + (c0 + 4) * 256].rearrange("p (c t) -> p c t", t=256)[:, :, a:a + 128]
                        wn = "xh"
                    copy_op(evac_eng(), dst, src, [bn], ([wn] if wn else XCALL[c0:c0 + 4]))

        if l == 0:
            for wb in wbs:
                if 1 <= wb <= 4:
                    load_block(wb)
        else:
            t0 = sl["in_off"] + s * 128
            deps = [f"x1:{name}:{g}:{c0}" for g in range(max(s - 1, 0) // 4, min(s + 4, nblk - 1) // 4 + 1) for c0 in range(0, KC, 4)]

            def lfn(e, sem, t0=t0):
                e.dma_start(out=XC[:], in_=x1T[:, :, t0:t0 + T].rearrange("c p t -> p c t")).then_inc(sem, 16)
                if lv:
                    e.dma_start(out=r2_3d(XH_O, KC, 256, 0, 128), in_=x1T[:, :, t0 - 128:t0].rearrange("c p t -> p c t")).then_inc(sem, 16)
                if rv:
                    e.dma_start(out=r2_3d(XH_O, KC, 256, 128, 256), in_=x1T[:, :, t0 + T:t0 + T + 128].rearrange("c p t -> p c t")).then_inc(sem, 16)
            P.dma("pool", "xld", lfn, 1 + int(lv) + int(rv), reads=deps, writes=XCALL + ["xh"])

        def proj_center(col0, narrow=False):
            slot, sn = wload(w_in[l, :, col0:col0 + 128], KC)
            bk, bn = bank()
            a, b = (cl, ch) if narrow else (0, T)
            mm_group(bk[:, a:b], bn, [(slot[:, k, :], XC[:, k, a:b]) for k in range(KC)], [sn] + XCALL)
            return slot, sn, bk, bn

        def proj_halo(slot, sn):
            bk, bn = bank()
            mm_group(bk[:, 0:256], bn, [(slot[:, k, :], XH(k, 0, 256)) for k in range(KC)], [sn, "xh"])
            return bk, bn

        def rope(dst_ap, wname, src_ap, bn, c0, n):
            raw, rn = tmp()
            P.op("act", lambda e: e.activation(out=raw[:, 0:n], in_=src_ap, func=AF.Copy), reads=[bn], writes=[rn])
            t1, t1n = tmp()
            P.op("dve", lambda e: e.tensor_tensor(out=t1[:, 0:n], in0=raw[:, 0:n], in1=COS[:, c0:c0 + n], op=ALU.mult),
                 reads=[rn, "cos"], writes=[t1n])
            t2, t2n = tmp()

            def f2(e):
                e.tensor_tensor(out=t2[0:64, 0:n], in0=raw[64:128, 0:n], in1=SIN[64:128, c0:c0 + n], op=ALU.mult)
                return e.tensor_tensor(out=t2[64:128, 0:n], in0=raw[0:64, 0:n], in1=SIN[0:64, c0:c0 + n], op=ALU.mult)
            P.op("dve", f2, reads=[rn, "sin"], writes=[t2n])
            P.op("dve", lambda e: e.tensor_tensor(out=dst_ap, in0=t1[:, 0:n], in1=t2[:, 0:n], op=ALU.add),
                 reads=[t1n, t2n], writes=[wname])

        for h in range(8):
            slot, sn, bk, bn = proj_center(h * 128, narrow=True)
            rope(qT(h), "q", bk[:, :], bn, 128, T)
        if l == 0:
            for wb in wbs:
                if not (1 <= wb <= 4):
                    load_block(wb)
        for kv in range(2):
            slot, sn, bk, bn = proj_center(K_OFF + kv * 128)
            bh, bhn = proj_halo(slot, sn)
            rope(kT(kv, 128, 640), "k", bk[:, :], bn, 128, T)
            if lv:
                rope(kT(kv, 0, 128), "k", bh[:, 0:128], bhn, 0, 128)
            if rv:
                rope(kT(kv, 640, 768), "k", bh[:, 128:256], bhn, 640, 128)
        for kv in range(2):
            slot, sn, bk, bn = proj_center(V_OFF + kv * 128)
            bh, bhn = proj_halo(slot, sn)
            copy_op("act", vT(kv, 128, 640), bk[:, :], [bn], ["vT"])
            if lv:
                copy_op("act", vT(kv, 0, 128), bh[:, 0:128], [bhn], ["vT"])
            if rv:
                copy_op("act", vT(kv, 640, 768), bh[:, 128:256], [bhn], ["vT"])
        for wb in wbs:
            bk, bn = bank()

            def vfn(e, bk=bk, wb=wb):
                e.transpose(bk[:, 0:128], vT(0, wb * 128, wb * 128 + 128, F32), IDN[:])
                return e.transpose(bk[:, 128:256], vT(1, wb * 128, wb * 128 + 128, F32), IDN[:])
            P.op("pe", vfn, reads=["vT", "idn"], writes=[bn])
            copy_op("dve", r2(V_O + wb * 256, 256), bk[:, 0:256], [bn], ["V"])
        for j in range(8):
            slot, sn, bk, bn = proj_center(U_OFF + j * 128)
            bh, bhn = proj_halo(slot, sn)
            copy_op("act", uT(j, 8, 520), bk[:, :], [bn], ["u"])
            copy_op("act", uT(j, 0, 8), bh[:, 120:128], [bhn], ["u"])
            copy_op("act", uT(j, 520, 528), bh[:, 128:136], [bhn], ["u"])
        def pool_chunk(j):
            gi = j // 2
            w = POOLW[gi]
            hh = w // 2
            cur = None
            curn = "u"
            m = 1
            ln = 528
            while m < w:
                nxt, nn = tmp()
                ln2 = ln - m

                def pf(e, cur=cur, nxt=nxt, m=m, ln2=ln2, j=j):
                    a = uT(j, 0, ln2, F32) if cur is None else cur[:, 0:ln2]
                    b = uT(j, m, m + ln2, F32) if cur is None else cur[:, m:m + ln2]
                    return e.tensor_tensor(out=nxt[:, 0:ln2], in0=a, in1=b, op=ALU.add)
                P.op("dve", pf, reads=[curn], writes=[nn])
                cur, curn, ln, m = nxt, nn, ln2, m * 2
            t3, t3n = tmp()
            P.op("dve", lambda e, cur=cur, t3=t3, gi=gi, hh=hh: e.tensor_tensor(out=t3[:, 0:T], in0=cur[:, 8 - hh:8 - hh + T], in1=RC[:, gi, :], op=ALU.mult),
                 reads=[curn, "rc"], writes=[t3n])
            P.op("dve", lambda e, t3=t3, j=j: e.tensor_tensor(out=uT(j, 8, 520), in0=t3[:, 0:T], in1=uT(j, 8, 520, F32), op=ALU.subtract),
                 reads=[t3n, "u"], writes=["u"])
        def pool_mix():
            i = st["wslot"] % NWS
            st["wslot"] += 1
            wm = WS[i]

            def pmfn(e, sem, wm=wm):
                e.dma_start(out=wm[:, :, :].rearrange("p (a b) c -> p a (b c)", b=2), in_=w_pm[l].rearrange("g (k p) c -> p (g k) c", p=128)).then_inc(sem, 16)
            P.dma("pool", f"w{i}", pmfn, 1, writes=[f"ws{i}"])
            wmv = wm[:, :, :].rearrange("p (a b) c -> p a (b c)", b=2)
            for g4 in range(4):
                for mo in range(2):
                    jo = g4 * 2 + mo
                    bk, bn = bank()
                    mm_group(bk[:, cl:ch], bn, [(wmv[:, g4 * 2 + ki, mo * 128:(mo + 1) * 128], uT(g4 * 2 + ki, 8 + cl, 8 + ch)) for ki in range(2)],
                             [f"ws{i}", "u"])
                    P.op("act", lambda e, bk=bk, jo=jo: e.activation(out=mxT(jo), in_=bk[:, :], func=AF.Identity, scale=PRM2[:, l * 8 + jo:l * 8 + jo + 1]),
                         reads=[bn, "prm2"], writes=["mx"])

        def kbias(wb):
            if not sl["ctype"]:
                return 0.0
            b = s - 1 + wb
            if b <= 1:
                return PRM2[:, 32:33]
            if b >= nblk - 2:
                return PRM2[:, 33:34]
            return 0.0

        steps = []
        for kv in range(2):
            for qi in range(4):
                wq = qi + 1
                kbs = [kb for kb in (wq - 1, wq, wq + 1) if kb in wbs]
                for kb in kbs:
                    steps.append((kv, qi, kb, kb == kbs[0], kb == kbs[-1]))
        accs = [((BANK[4], "bank4"), (BANK[5], "bank5")), ((BANK[6], "bank6"), (BANK[7], "bank7"))]
        pend = []

        def emit_st(step):
            kv, qi, kb, first, last = step
            wq = qi + 1
            bk, bn = bank((0, 1, 2, 3))
            q4 = R2[:, Q_O + kv * 4 * T: Q_O + (kv * 4 + 4) * T].rearrange("p (h t) -> p h t", t=T)[:, :, qi * 128:(qi + 1) * 128]

            def sfn(e, bk=bk, kb=kb, kv=kv, q4=q4):
                return e.matmul(bk[:, :].rearrange("p (h t) -> p h t", t=128), lhsT=kT(kv, kb * 128, kb * 128 + 128), rhs=q4,
                                start=True, stop=True)
            P.op("pe", sfn, reads=["k", "q", "cstr"], writes=[bn])
            pt, ptn = ptile()
            kb_b = kbias(kb)
            rd = [bn] + (["prm2"] if not isinstance(kb_b, float) else [])
            P.op("act", lambda e, pt=pt, bk=bk, kb_b=kb_b: e.activation(out=pt[:], in_=bk[:, :], func=AF.Exp, bias=kb_b, scale=SCALE),
                 reads=rd, writes=[ptn])
            if kb != wq:
                P.op("pool", lambda e, pt=pt, which=(0 if kb < wq else 1): e.tensor_tensor(out=pt[:], in0=pt[:], in1=MB(which), op=ALU.mult),
                     reads=[ptn, "cstr"], writes=[ptn])
            return pt, ptn

        def emit_pv(step, pt, ptn):
            kv, qi, kb, first, last = step
            (ob, obn), (db, dbn) = accs[(kv * 4 + qi) % 2]
            mm_group(ob[:, :], obn, [(Vtok(kb, kv), pt[:])], ["V", ptn], start=first, stop=last)
            mm_group(db[:, :], dbn, [(ONESR, pt[:])], ["cstr", ptn], start=first, stop=last)
            if last:
                rt, rtn = tmp()
                def addsink(e):
                    ins = None
                    for h in range(4):
                        ins = e.tensor_scalar(out=rt[:, h * 128:(h + 1) * 128], in0=db[:, h * 128:(h + 1) * 128],
                                              scalar1=ESINK[:, kv * 4 + h:kv * 4 + h + 1], scalar2=None, op0=ALU.add)
                    return ins
                P.op("dve", addsink, reads=[dbn, "esink"], writes=[rtn])
                P.op("dve", lambda e: e.reciprocal(out=rt[:, 0:T], in_=rt[:, 0:T]), reads=[rtn], writes=[rtn])
                dst = R2[:, XH_O + kv * 4 * T: XH_O + (kv * 4 + 4) * T].rearrange("p (h t) -> p h t", t=T)[:, :, qi * 128:(qi + 1) * 128]
                P.op("dve", lambda e: e.tensor_tensor(out=dst, in0=ob[:, :].rearrange("p (h t) -> p h t", t=128),
                                                      in1=rt[:, 0:T].rearrange("p (h t) -> p h t", t=128), op=ALU.mult),
                     reads=[obn, rtn], writes=["o"])

        SKEW = 3
        for idx, step in enumerate(steps):
            pt, ptn = emit_st(step)
            pend.append((step, pt, ptn))
            if len(pend) > SKEW:
                emit_pv(*pend.pop(0))
        while pend:
            emit_pv(*pend.pop(0))

        for m in range(16):
            slot, sn = wload(w_in[l, :, G_OFF + m * 128: G_OFF + (m + 1) * 128], KC)
            bg, bgn = bank()
            mm_group(bg[:, cl:ch], bgn, [(slot[:, k, :], XC[:, k, cl:ch]) for k in range(KC)], [sn] + XCALL)
            slot, sn = wload(w_ap[l, :, m * 128:(m + 1) * 128], 8)
            ba, ban = bank()
            mm_group(ba[:, cl:ch], ban, [(slot[:, k, :], oT(k, cl, ch)) for k in range(8)], [sn, "o"])
            t1, t1n = tmp()
            P.op("act", lambda e, t1=t1, bg=bg: e.activation(out=t1[:, 0:T], in_=bg[:, :], func=AF.Sigmoid), reads=[bgn], writes=[t1n])
            P.op("dve", lambda e, t1=t1, ba=ba, m=m: e.tensor_tensor(out=mixT(m), in0=t1[:, 0:T], in1=ba[:, :], op=ALU.mult),
                 reads=[t1n, ban], writes=[f"mix{m}"])
            if m % 2 == 1:
                pool_chunk(m // 2)
        pool_mix()
        for m in range(16):
            slot, sn = wload(w_pp[l, :, m * 128:(m + 1) * 128], 8)
            bb, bbn = bank()
            mm_group(bb[:, cl:ch], bbn, [(slot[:, k, :], mxT(k, cl, ch)) for k in range(8)], [sn, "mx"])
            slot, sn = wload(w_in[l, :, G_OFF + D + m * 128: G_OFF + D + (m + 1) * 128], KC)
            bg2, bg2n = bank()
            mm_group(bg2[:, cl:ch], bg2n, [(slot[:, k, :], XC[:, k, cl:ch]) for k in range(KC)], [sn] + XCALL)
            t2, t2n = tmp()
            P.op("act", lambda e, t2=t2, bg2=bg2: e.activation(out=t2[:, 0:T], in_=bg2[:, :], func=AF.Sigmoid), reads=[bg2n], writes=[t2n])
            P.op("dve", lambda e, t2=t2, bb=bb: e.tensor_tensor(out=t2[:, 0:T], in0=t2[:, 0:T], in1=bb[:, :], op=ALU.mult),
                 reads=[t2n, bbn], writes=[t2n])
            P.op("dve", lambda e, t2=t2, m=m: e.tensor_tensor(out=mixT(m), in0=mixT(m, F32), in1=t2[:, 0:T], op=ALU.add),
                 reads=[t2n, f"mix{m}"], writes=[f"mix{m}"])

        def res_ln(wsrc_fn, nk_list, rhs_fn, rhs_name, goff, boff, final=False):
            sb, sbn = BANK[6], "bank6"
            qb_, qbn = BANK[7], "bank7"
            pend_st = []

            def stats_mm(m, sq, sqn):
                mm_group(sb[:, cl:ch], sbn, [(ONESR, XC[:, m, cl:ch])], ["cstr", XCALL[m]], start=(m == 0), stop=(m == 15))
                mm_group(qb_[:, cl:ch], qbn, [(ONESR, sq[:, cl:ch])], ["cstr", sqn], start=(m == 0), stop=(m == 15))

            for m in range(16):
                bk, bn = bank()
                k0 = 0
                nparts = len(nk_list)
                for pi, nk in enumerate(nk_list):
                    slot, sn = wload(wsrc_fn(m, k0, nk), nk)
                    sub = 4 if m == 0 else nk
                    for a0 in range(0, nk, sub):
                        a1 = min(a0 + sub, nk)
                        mm_group(bk[:, cl:ch], bn, [(slot[:, k, :], rhs_fn(k0 + k)) for k in range(a0, a1)],
                                 [sn] + [f"{rhs_name}{k0 + k}" for k in range(a0, a1)],
                                 start=(pi == 0 and a0 == 0), stop=(pi == nparts - 1 and a1 == nk))
                    k0 += nk
                P.op("dve", lambda e, bk=bk, m=m: e.scalar_tensor_tensor(out=XC[:, m, :], in0=XC[:, m, :], scalar=ALPHA, in1=bk[:, :],
                                                                          op0=ALU.mult, op1=ALU.add),
                     reads=[bn, XCALL[m]], writes=[XCALL[m]])
                sq, sqn = SQ[m % 2], f"sq{m % 2}"
                P.op("act", lambda e, sq=sq, m=m: e.activation(out=sq[:], in_=XC[:, m, :], func=AF.Square),
                     reads=[XCALL[m]], writes=[sqn])
                pend_st.append((m, sq, sqn))
                if len(pend_st) > 1:
                    stats_mm(*pend_st.pop(0))
            while pend_st:
                stats_mm(*pend_st.pop(0))
            mean, rstd = LNT
            negmean = mean[:].bitcast(F32R)
            P.op("act", lambda e: e.activation(out=negmean, in_=sb[:, :], func=AF.Copy, scale=-1.0 / D), reads=[sbn], writes=["lnt0"])
            t1, t1n = tmp()
            P.op("act", lambda e: e.activation(out=t1[:, 0:T], in_=sb[:, :], func=AF.Square, scale=1.0 / D), reads=[sbn], writes=[t1n])
            P.op("dve", lambda e: e.scalar_tensor_tensor(out=rstd[:], in0=qb_[:, :], scalar=1.0 / D, in1=t1[:, 0:T], op0=ALU.mult, op1=ALU.subtract),
                 reads=[qbn, t1n], writes=["lnt1"])
            P.op("act", lambda e: e.activation(out=rstd[:], in_=rstd[:], func=AF.Sqrt, bias=EPSC), reads=["lnt1", "prm2"], writes=["lnt1"])
            P.op("dve", lambda e: e.reciprocal(out=rstd[:], in_=rstd[:]), reads=["lnt1"], writes=["lnt1"])
            for m in range(16):
                t2, t2n = tmp()
                cb, cbn = bank((0, 1, 2, 3, 4, 5))
                mm_group(cb[:, cl:ch], cbn, [(IDR, XC[:, m, cl:ch]), (IDR, negmean[:, cl:ch])], ["cstr", XCALL[m], "lnt0"])
                P.op("dve", lambda e, t2=t2, cb=cb: e.tensor_tensor(out=t2[:, 0:T], in0=cb[:, :], in1=rstd[:], op=ALU.mult),
                     reads=[cbn, "lnt1"], writes=[t2n])
                P.op("act", lambda e, t2=t2, m=m: e.activation(out=XC[:, m, :], in_=t2[:, 0:T], func=AF.Identity,
                                                              scale=PRM[:, goff + l * 16 + m: goff + l * 16 + m + 1],
                                                              bias=PRM[:, boff + l * 16 + m: boff + l * 16 + m + 1]),
                     reads=[t2n, "prm"], writes=[XCALL[m]])

        if next_unit is not None:
            prefetch(*next_unit)
        res_ln(lambda m, k0, nk: w_out[l, k0 * 128:(k0 + nk) * 128, m * 128:(m + 1) * 128], [16], lambda k: mixT(k, F32R, cl, ch), "mix", 0, 32)

        for j in range(NJ):
            slot, sn = wload(w_f1[l, :, j * 128:(j + 1) * 128], KC)
            b1, b1n = bank()
            mm_group(b1[:, cl:ch], b1n, [(slot[:, k, :], XC[:, k, cl:ch]) for k in range(KC)], [sn] + XCALL)
            slot, sn = wload(w_f1[l, :, DFF + j * 128: DFF + (j + 1) * 128], KC)
            b2, b2n = bank()
            mm_group(b2[:, cl:ch], b2n, [(slot[:, k, :], XC[:, k, cl:ch]) for k in range(KC)], [sn] + XCALL)
            P.op("act", lambda e, b1=b1, j=j: e.activation(out=gT(j), in_=b1[:, :], func=AF.Silu), reads=[b1n], writes=[f"g{j}"])
            P.op("dve", lambda e, b2=b2, j=j: e.tensor_tensor(out=gT(j), in0=gT(j, F32), in1=b2[:, :], op=ALU.mult),
                 reads=[b2n, f"g{j}"], writes=[f"g{j}"])
        res_ln(lambda m, k0, nk: w_f2[l, k0 * 128:(k0 + nk) * 128, m * 128:(m + 1) * 128], [16, 16, 12], lambda k: gT(k, F32R, cl, ch), "g", 64, 96, final=(l == DEPTH - 1))

        if l == 0:
            if sl["ctype"] and s == 0:
                P.op("dve", lambda e: e.tensor_scalar(out=XC[:, :, 0:256], in0=XC[:, :, 0:256], scalar1=PRM2[:, 34:35], scalar2=None, op0=ALU.mult),
                     reads=XCALL + ["prm2"], writes=XCALL)
            if sl["ctype"] and s + 4 >= nblk:
                P.op("dve", lambda e: e.tensor_scalar(out=XC[:, :, 256:512], in0=XC[:, :, 256:512], scalar1=PRM2[:, 35:36], scalar2=None, op0=ALU.mult),
                     reads=XCALL + ["prm2"], writes=XCALL)
            t0 = sl["in_off"] + s * 128
            for c0 in range(0, KC, 4):
                def sfn(e, sem, t0=t0, c0=c0):
                    e.dma_start(out=x1T[c0:c0 + 4, :, t0:t0 + T].rearrange("c p t -> p c t"), in_=XC[:, c0:c0 + 4, :].bitcast(F32)).then_inc(sem, 16)
                P.dma("sp", f"x1st{c0}", sfn, 1, reads=XCALL[c0:c0 + 4], writes=[f"x1:{name}:{s // 4}:{c0}"])
        else:
            for qi in range(4):
                STG, sgn = stage()
                for c0 in range(0, KC, 4):
                    bk, bn = bank()

                    def tfn3(e, bk=bk, c0=c0, qi=qi):
                        ins = None
                        for c in range(4):
                            ins = e.transpose(bk[:, c * 128:(c + 1) * 128], XC[:, c0 + c, qi * 128:(qi + 1) * 128].bitcast(F32), IDN[:])
                        return ins
                    P.op("pe", tfn3, reads=XCALL[c0:c0 + 4] + ["idn"], writes=[bn])
                    copy_op(evac_eng(), STG[:, c0 * 128:(c0 + 4) * 128], bk[:, :], [bn], [sgn])
                row0 = sl["out_off"] + (s - sl["out_b0"] + qi) * 128

                def ofn(e, sem, row0=row0, STG=STG):
                    e.dma_start(out=yout[row0:row0 + 128, :], in_=STG[:]).then_inc(sem, 16)
                P.dma("sp", "yst" + sgn, ofn, 1, reads=[sgn], writes=["yout" + sgn])

    units = [(l, sl, s) for l in range(DEPTH) for sl in slots for s in (sl["l1"] if l == 0 else sl["l2"])]
    for i, (l, sl, s) in enumerate(units):
        if i == 0 or units[i - 1][0] != l:
            layer_setup(l)
        group(l, sl, s, units[i + 1] if i + 1 < len(units) else None)
    P.finalize()

    names = P.sem_names()
    with ExitStack() as es:
        sems = {n: es.enter_context(nc.semaphore(n)) for n in names}
        block = es.enter_context(nc.Block())

        @block.tensor
        def _(e):
            P.emit("pe", e, sems)

        @block.scalar
        def _(e):
            P.emit("act", e, sems)

        @block.vector
        def _(e):
            P.emit("dve", e, sems)

        @block.gpsimd
        def _(e):
            P.emit("pool", e, sems)

        @block.sync
        def _(e):
            P.emit("sp", e, sems)
            for k_ in ("yststg0", "yststg1"):
                if k_ in P.dma_cnt:
                    e.wait_ge(sems[k_], P.dma_cnt[k_])
    return nc


def _tables(pos_list):
    pos = np.asarray(pos_list, dtype=np.float32)
    half = 64
    inv = (10000.0 ** (-np.arange(half, dtype=np.float32) / half)).astype(np.float32)
    ang = pos[None, :] * inv[:, None]
    c = np.cos(ang).astype(np.float32)
    s_ = np.sin(ang).astype(np.float32)
    return np.concatenate([c, c], 0), np.concatenate([s_, -s_], 0)


def _rc(pos, S):
    out = np.zeros((4, len(pos)), np.float32)
    pos = np.asarray(pos)
    for gi, w in enumerate(POOLW):
        lo = np.clip(pos - w // 2, 0, S)
        hi = np.clip(pos + (w - w // 2), 0, S)
        cnt = (hi - lo).astype(np.float32)
        out[gi] = np.where(cnt > 0, 1.0 / np.maximum(cnt, 1.0), 0.0)
    return out


def _consts():
    cst = np.zeros((128, 1280), np.float32)
    cst[:, 0:128] = np.eye(128, dtype=np.float32)
    cst[:, 128:256] = 1.0
    j = np.arange(128)[:, None]
    i = np.arange(128)[None, :]
    prev = np.where(j >= i, 1.0, 0.0).astype(np.float32)
    nxt = np.where(j <= i, 1.0, 0.0).astype(np.float32)
    cst[:, 256:768] = np.tile(prev, (1, 4))
    cst[:, 768:1280] = np.tile(nxt, (1, 4))
    return cst


def _fm(v):
    L = v.shape[0]
    n = v.shape[1] // 128
    return np.ascontiguousarray(v.reshape(L, n, 128).transpose(2, 0, 1).reshape(128, L * n))


_NC_CACHE = {}


def kernel(x_prompt, x_sample, w_in, sink, w_attn_proj, w_pool_mix, pool_scale, w_pool_proj, w_out,
           ln1_g, ln1_b, w_ffn_in, w_ffn_out, ln2_g, ln2_b):
    f = lambda a: np.ascontiguousarray(np.asarray(a, dtype=np.float32))
    x_prompt, x_sample = f(x_prompt), f(x_sample)
    ncores = 8
    cst = _consts()
    prm = np.concatenate([_fm(f(ln1_g)), _fm(f(ln1_b)), _fm(f(ln2_g)), _fm(f(ln2_b))], axis=1)
    shared = dict(w_in=f(w_in), w_attn_proj=f(w_attn_proj), w_pool_mix=f(w_pool_mix), w_pool_proj=f(w_pool_proj),
                  w_out=f(w_out), w_ffn_in=f(w_ffn_in), w_ffn_out=f(w_ffn_out), prm=np.ascontiguousarray(prm), cst=cst)
    sinkb = np.broadcast_to(f(sink).reshape(1, 16), (128, 16))
    in_maps = []
    for c in range(ncores):
        sq, half = divmod(c, 2)
        xin = np.zeros((6656, D), np.float32)
        xin[0:2048] = x_prompt[2 * c]
        xin[2048:4096] = x_prompt[2 * c + 1]
        w0 = half * 2048 - 256
        lo, hi = max(w0, 0), min(w0 + 2560, 4096)
        xin[4096 + (lo - w0): 4096 + (hi - w0)] = x_sample[sq, lo:hi]
        posC = np.arange(w0, w0 + 2560)
        posAB = np.arange(2048)
        cAB, sAB = _tables(posAB)
        cC, sC = _tables(posC)
        cosT = np.concatenate([cAB, cC], 1)
        sinT = np.concatenate([sAB, sC], 1)
        rc = np.concatenate([_rc(posAB, 2048), _rc(posC, 4096)], 1)
        rcT = np.ascontiguousarray(np.broadcast_to(rc[None], (128, 4, 4608)))
        prm2 = np.zeros((128, 40), np.float32)
        prm2[:, 0:16] = _fm(f(pool_scale))
        prm2[:, 16:32] = sinkb
        lvalid, rvalid = (half == 1), (half == 0)
        prm2[:, 32] = 0.0 if lvalid else NEG
        prm2[:, 33] = 0.0 if rvalid else NEG
        prm2[:, 34] = 1.0 if lvalid else 0.0
        prm2[:, 35] = 1.0 if rvalid else 0.0
        prm2[:, 36] = EPS
        m = dict(shared)
        m.update(xin=xin, cosT=np.ascontiguousarray(cosT), sinT=np.ascontiguousarray(sinT), rcT=rcT, prm2=prm2)
        in_maps.append(m)
    if "full" not in _NC_CACHE:
        _NC_CACHE["full"] = build_program(FULL_CFG)
    nc = _NC_CACHE["full"]
    res = run_bass_kernel_spmd(nc, in_maps, core_ids=list(range(ncores)))
    y_prompt = np.empty((16, 2048, D), np.float32)
    y_sample = np.empty((4, 4096, D), np.float32)
    for c in range(ncores):
        y = np.asarray(res.results[c]["yout"])
        sq, half = divmod(c, 2)
        y_prompt[2 * c] = y[0:2048]
        y_prompt[2 * c + 1] = y[2048:4096]
        y_sample[sq, half * 2048:(half + 1) * 2048] = y[4096:6144]
    return (y_prompt, y_sample)
```

```python
import math
from contextlib import ExitStack
import numpy as np
import concourse.bass as bass
import concourse.mybir as mybir
from concourse.bass_utils import run_bass_kernel_spmd

F32 = mybir.dt.float32
F32R = mybir.dt.float32r
AF = mybir.ActivationFunctionType
ALU = mybir.AluOpType

D = 2048
KC = 16
INW = 6656
DFF = 5632
NJ = 44
DEPTH = 2
ALPHA = (2.0 * DEPTH) ** 0.25
EPS = 1e-5
SCALE = 128 ** -0.5
NEG = -1e30
K_OFF, V_OFF, U_OFF, G_OFF = 1024, 1280, 1536, 2560
POOLW = (2, 4, 8, 16)
T = 512

FULL_SLOTS = [
    dict(name="A", in_off=0, nblk=16, l1=[0, 4, 8, 12], l2=[0, 4, 8, 12], out_off=0, out_b0=0, tbl_off=0, ctype=False),
    dict(name="B", in_off=2048, nblk=16, l1=[0, 4, 8, 12], l2=[0, 4, 8, 12], out_off=2048, out_b0=0, tbl_off=0, ctype=False),
    dict(name="C", in_off=4096, nblk=20, l1=[0, 4, 8, 12, 16], l2=[2, 6, 10, 14], out_off=4096, out_b0=2, tbl_off=2048, ctype=True),
]
FULL_CFG = dict(slots=FULL_SLOTS, ntok_in=6656, ntok_out=6144, ntbl=4608)


class Res:
    __slots__ = ("name", "lw", "rd", "al")

    def __init__(self, name):
        self.name = name
        self.lw = None
        self.rd = {}
        self.al = []


class Op:
    __slots__ = ("eng", "fn", "deps", "signal", "sem", "val", "is_dma", "ndma")

    def __init__(self, eng, fn):
        self.eng = eng
        self.fn = fn
        self.deps = []
        self.signal = False
        self.sem = None
        self.val = 0
        self.is_dma = False
        self.ndma = 0


class Plan:
    ENGS = ("pe", "act", "dve", "pool", "sp")
    SYNC_SAME = ("act", "dve", "pool")

    def __init__(self):
        self.q = {e: [] for e in self.ENGS}
        self.res = {}
        self.dma_cnt = {}

    def r(self, name):
        x = self.res.get(name)
        if x is None:
            x = self.res[name] = Res(name)
        return x

    def alias(self, a, b):
        ra, rb = self.r(a), self.r(b)
        ra.al.append(rb)
        rb.al.append(ra)

    def _add(self, op, reads, writes):
        deps = []
        for n in reads:
            x = self.r(n)
            if x.lw is not None:
                deps.append(x.lw)
            for y in x.al:
                if y.lw is not None:
                    deps.append(y.lw)
        for n in writes:
            x = self.r(n)
            for y in [x] + x.al:
                if y.lw is not None:
                    deps.append(y.lw)
                deps.extend(y.rd.values())
        for d in deps:
            if d is op:
                continue
            if d.is_dma or d.eng != op.eng or (op.eng in self.SYNC_SAME and not op.is_dma):
                d.signal = True
                op.deps.append(d)
        for n in reads:
            self.r(n).rd[op.eng if not op.is_dma else ("dma", op.sem)] = op
        for n in writes:
            x = self.r(n)
            x.lw = op
            x.rd = {}
        self.q[op.eng].append(op)
        return op

    def op(self, eng, fn, reads=(), writes=()):
        return self._add(Op(eng, fn), reads, writes)

    def dma(self, eng, semkey, fn, ndma, reads=(), writes=()):
        o = Op(eng, fn)
        o.is_dma = True
        o.ndma = ndma
        o.sem = semkey
        self.dma_cnt[semkey] = self.dma_cnt.get(semkey, 0) + 16 * ndma
        o.val = self.dma_cnt[semkey]
        o.signal = True
        return self._add(o, reads, writes)

    def finalize(self):
        for e in self.ENGS:
            c = 0
            for o in self.q[e]:
                if not o.is_dma and o.signal:
                    c += 1
                    o.sem = "eng_" + e
                    o.val = c

    def sem_names(self):
        s = set(self.dma_cnt.keys())
        for e in self.ENGS:
            s.add("eng_" + e)
        return sorted(s)

    def emit(self, eng, e, sems):
        waited = {}
        for o in self.q[eng]:
            need = {}
            for d in o.deps:
                if need.get(d.sem, 0) < d.val:
                    need[d.sem] = d.val
            for s, v in need.items():
                if waited.get(s, 0) < v:
                    e.wait_ge(sems[s], v)
                    waited[s] = v
            if o.is_dma:
                o.fn(e, sems[o.sem])
            else:
                ins = o.fn(e)
                if o.signal:
                    ins.then_inc(sems[o.sem], 1)


def build_program(cfg):
    slots = cfg["slots"]
    NTI, NTO, NTB = cfg["ntok_in"], cfg["ntok_out"], cfg["ntbl"]
    nc = bass.Bass("TRN2", target_bir_lowering=False)

    def din(name, shape):
        return nc.dram_tensor(name, list(shape), F32, kind="ExternalInput").ap()

    xin = din("xin", [NTI, D])
    w_in = din("w_in", [DEPTH, D, INW])
    w_ap = din("w_attn_proj", [DEPTH, 1024, D])
    w_pm = din("w_pool_mix", [DEPTH, 4, 256, 256])
    w_pp = din("w_pool_proj", [DEPTH, 1024, D])
    w_out = din("w_out", [DEPTH, D, D])
    w_f1 = din("w_ffn_in", [DEPTH, D, 2 * DFF])
    w_f2 = din("w_ffn_out", [DEPTH, DFF, D])
    prm_d = din("prm", [128, 128])
    prm2_d = din("prm2", [128, 40])
    cos_d = din("cosT", [128, NTB])
    sin_d = din("sinT", [128, NTB])
    rc_d = din("rcT", [128, 4, NTB])
    cst_d = din("cst", [128, 1280])
    yout = nc.dram_tensor("yout", [NTO, D], F32, kind="ExternalOutput").ap()
    x1T = nc.dram_tensor("x1T", [KC, 128, NTI], F32, kind="Internal").ap()

    NWS = 4
    WS = [nc.alloc_sbuf_tensor(f"ws{i}", [128, 16, 128], F32R) for i in range(NWS)]
    XC = nc.alloc_sbuf_tensor("xc", [128, KC, T], F32R)
    R2 = nc.alloc_sbuf_tensor("r2", [128, 22528], F32R)
    SQ = [nc.alloc_sbuf_tensor(f"sq{i}", [128, T], F32R) for i in range(2)]
    COS = nc.alloc_sbuf_tensor("cos", [128, 768], F32)
    SIN = nc.alloc_sbuf_tensor("sin", [128, 768], F32)
    RC = nc.alloc_sbuf_tensor("rc", [128, 4, T], F32)
    NTMP = 3
    TMP = [nc.alloc_sbuf_tensor(f"tmp{i}", [128, 528], F32) for i in range(NTMP)]
    NPT = 2
    PT = [nc.alloc_sbuf_tensor(f"pt{i}", [128, T], F32R) for i in range(NPT)]
    LNT = [nc.alloc_sbuf_tensor(f"lnt{i}", [128, T], F32) for i in range(2)]
    STGS = [nc.alloc_sbuf_tensor(f"stg{i}", [128, D], F32) for i in range(2)]
    IDN = nc.alloc_sbuf_tensor("idn", [128, 128], F32)
    CSTR = nc.alloc_sbuf_tensor("cstr", [128, 1280], F32R)
    PRM = nc.alloc_sbuf_tensor("prm_s", [128, 128], F32)
    PRM2 = nc.alloc_sbuf_tensor("prm2_s", [128, 40], F32)
    ESINK = nc.alloc_sbuf_tensor("esink", [128, 8], F32)
    BANK = [nc.alloc_psum_tensor(f"bank{i}", [128, T], F32) for i in range(8)]

    IDR = CSTR[:, 0:128]
    EPSC = PRM2[:, 36:37]
    ONESR = CSTR[:, 128:256]

    def MB(which):
        return CSTR[:, 256 + which * 512: 256 + (which + 1) * 512]

    def r2(off, n, dt=F32R):
        a = R2[:, off:off + n]
        return a.bitcast(dt) if dt is not F32R else a

    def r2_3d(off, nch, w, a, b, dt=F32R):
        v = R2[:, off:off + nch * w].rearrange("p (c t) -> p c t", t=w)[:, :, a:b]
        return v.bitcast(dt) if dt is not F32R else v

    Q_O, K_O, VT_O, V_O, U_O, MX_O, XH_O = 0, 4096, 5632, 7168, 8704, 12928, 17024

    def qT(h, a=0, b=T):
        return r2(Q_O + h * T + a, b - a)

    def kT(kv, a, b):
        return r2(K_O + kv * 768 + a, b - a)

    def vT(kv, a, b, dt=F32R):
        return r2(VT_O + kv * 768 + a, b - a, dt)

    def Vtok(wb, kv):
        return r2(V_O + wb * 256 + kv * 128, 128)

    def uT(j, a, b, dt=F32R):
        return r2(U_O + j * 528 + a, b - a, dt)

    def mxT(j, a=0, b=T):
        return r2(MX_O + j * T + a, b - a)

    def XH(k, a, b):
        return r2(XH_O + k * 256 + a, b - a)

    def oT(k, a=0, b=T):
        return r2(XH_O + k * T + a, b - a)

    def mixT(m, dt=F32R, a=0, b=T):
        return r2(m * T + a, b - a, dt)

    def gT(j, dt=F32R, a=0, b=T):
        return r2(j * T + a, b - a, dt)

    P = Plan()
    XCALL = [f"xc{k}" for k in range(KC)]
    for m_ in range(16):
        for a in ("q", "k", "vT", "V"):
            P.alias(f"mix{m_}", a)
    for j_ in range(NJ):
        for a in ("q", "k", "vT", "V", "u", "mx", "xh", "o"):
            P.alias(f"g{j_}", a)
        if j_ < 16:
            P.alias(f"g{j_}", f"mix{j_}")
    P.alias("xh", "o")

    st = dict(tmp=0, pt=0, rot=0, wslot=0, acc=0, evac=0, stg=0)

    def stage():
        i = st["stg"] % 2
        st["stg"] += 1
        return STGS[i], f"stg{i}"

    def tmp():
        i = st["tmp"] % NTMP
        st["tmp"] += 1
        return TMP[i], f"tmp{i}"

    def ptile():
        i = st["pt"] % 4
        st["pt"] += 1
        return (PT[i], f"pt{i}") if i < 2 else (SQ[i - 2], f"sq{i - 2}")

    def bank(pool=(0, 1, 2, 3, 4, 5)):
        i = pool[st["rot"] % len(pool)]
        st["rot"] += 1
        return BANK[i], f"bank{i}"

    def wload(src_ap, nk, reads=()):
        i = st["wslot"] % NWS
        st["wslot"] += 1
        slot = WS[i]

        def fn(e, sem, slot=slot, src_ap=src_ap, nk=nk):
            e.dma_start(out=slot[:, 0:nk, :], in_=src_ap.rearrange("(k p) c -> p k c", p=128)).then_inc(sem, 16)
        P.dma("pool", f"w{i}", fn, 1, reads=list(reads), writes=[f"ws{i}"])
        return slot, f"ws{i}"

    def mm_group(out_ap, bname, pairs, reads, start=True, stop=True, extra_w=()):
        def fn(e, out_ap=out_ap, pairs=pairs, start=start, stop=stop):
            ins = None
            n = len(pairs)
            for i, (l, r) in enumerate(pairs):
                ins = e.matmul(out_ap, lhsT=l, rhs=r, start=(start and i == 0), stop=(stop and i == n - 1))
            return ins
        return P.op("pe", fn, reads=list(reads), writes=[bname] + list(extra_w))

    def c_fn(e, sem):
        e.dma_start(out=IDN[:], in_=cst_d[:, 0:128]).then_inc(sem, 16)
        e.dma_start(out=PRM[:], in_=prm_d).then_inc(sem, 16)
        e.dma_start(out=PRM2[:], in_=prm2_d).then_inc(sem, 16)
    P.dma("sp", "cst", c_fn, 3, writes=["idn", "prm", "prm2"])

    def c2_fn(e, sem):
        e.dma_start(out=CSTR[:], in_=cst_d).then_inc(sem, 16)
    P.dma("pool", "cst2", c2_fn, 1, writes=["cstr"])

    def evac_eng():
        st["evac"] += 1
        return "act" if st["evac"] % 2 else "dve"

    def copy_op(eng, out_ap, in_ap, reads, writes):
        if eng == "act":
            return P.op("act", lambda e, o=out_ap, i=in_ap: e.activation(out=o, in_=i, func=AF.Copy), reads=reads, writes=writes)
        return P.op("dve", lambda e, o=out_ap, i=in_ap: e.tensor_copy(out=o, in_=i), reads=reads, writes=writes)

    def layer_setup(l):
        P.op("act", lambda e: e.activation(out=ESINK[:], in_=PRM2[:, 16 + l * 8:16 + l * 8 + 8], func=AF.Exp),
             reads=["prm2"], writes=["esink"])

    pre = {"tbl": None, "blk": {}}

    def prep_tables(l, sl, s):
        nblk = sl["nblk"]
        toff = sl["tbl_off"]
        lo_b, hi_b = max(s - 1, 0), min(s + 5, nblk)

        def tfn(e, sem):
            c0 = (lo_b - (s - 1)) * 128
            n = (hi_b - lo_b) * 128
            e.dma_start(out=COS[:, c0:c0 + n], in_=cos_d[:, toff + lo_b * 128: toff + hi_b * 128]).then_inc(sem, 16)
            e.dma_start(out=SIN[:, c0:c0 + n], in_=sin_d[:, toff + lo_b * 128: toff + hi_b * 128]).then_inc(sem, 16)
            e.dma_start(out=RC[:], in_=rc_d[:, :, toff + s * 128: toff + s * 128 + T]).then_inc(sem, 16)
        P.dma("sp", "tbl", tfn, 3, writes=["cos", "sin", "rc"])

    def block_dma(sl, s, wb):
        row0 = sl["in_off"] + (s - 1 + wb) * 128
        STG, sgn = stage()

        def ifn(e, sem, row0=row0, STG=STG):
            e.dma_start(out=STG[:], in_=xin[row0:row0 + 128, :]).then_inc(sem, 16)
        P.dma("sp", sgn, ifn, 1, writes=[sgn])
        return STG, sgn

    def prefetch(l, sl, s):
        prep_tables(l, sl, s)
        pre["tbl"] = (l, sl["name"], s)
        if l == 0:
            for wb in (1, 2):
                pre["blk"][(l, sl["name"], s, wb)] = block_dma(sl, s, wb)

    def group(l, sl, s, next_unit=None):
        nblk = sl["nblk"]
        lv = s >= 1
        rv = s + 4 < nblk
        wbs = [wb for wb in range(6) if 0 <= s - 1 + wb < nblk]
        toff = sl["tbl_off"]
        name = sl["name"]

        cl, ch = 0, T
        if l == 0 and sl["ctype"]:
            if s == 0:
                cl = 128
            elif s + 4 >= nblk:
                ch = 384
        key = (l, name, s)
        if pre["tbl"] != key:
            prep_tables(l, sl, s)

        if not lv:
            P.op("dve", lambda e: e.memset(r2_3d(XH_O, KC, 256, 0, 128, F32), 0.0), writes=["xh"])
        if not rv:
            P.op("dve", lambda e: e.memset(r2_3d(XH_O, KC, 256, 128, 256, F32), 0.0), writes=["xh"])

        def load_block(wb):
                if (l, name, s, wb) in pre["blk"]:
                    STG, sgn = pre["blk"].pop((l, name, s, wb))
                else:
                    STG, sgn = block_dma(sl, s, wb)
                for c0 in range(0, KC, 4):
                    bk, bn = bank()

                    def tfn2(e, bk=bk, c0=c0, STG=STG):
                        ins = None
                        for c in range(4):
                            ins = e.transpose(bk[:, c * 128:(c + 1) * 128], STG[:, (c0 + c) * 128:(c0 + c + 1) * 128], IDN[:])
                        return ins
                    P.op("pe", tfn2, reads=[sgn, "idn"], writes=[bn])
                    src = bk[:, :].rearrange("p (c t) -> p c t", t=128)
                    if 1 <= wb <= 4:
                        dst = XC[:, c0:c0 + 4, (wb - 1) * 128: wb * 128]
                        wn = None
                    else:
                        a = 0 if wb == 0 else 128
                        dst = R2[:, XH_O + c0 * 256: XH_O + (c0 + 4) * 256].rearrange("p (c t) -> p c t", t=256)[:, :, a:a + 128]
                        wn = "xh"
                    copy_op(evac_eng(), dst, src, [bn], ([wn] if wn else XCALL[c0:c0 + 4]))

        if l == 0:
            for wb in wbs:
                if 1 <= wb <= 4:
                    load_block(wb)
        else:
            t0 = sl["in_off"] + s * 128
            deps = [f"x1:{name}:{g}:{c0}" for g in range(max(s - 1, 0) // 4, min(s + 4, nblk - 1) // 4 + 1) for c0 in range(0, KC, 4)]

            def lfn(e, sem, t0=t0):
                e.dma_start(out=XC[:], in_=x1T[:, :, t0:t0 + T].rearrange("c p t -> p c t")).then_inc(sem, 16)
                if lv:
                    e.dma_start(out=r2_3d(XH_O, KC, 256, 0, 128), in_=x1T[:, :, t0 - 128:t0].rearrange("c p t -> p c t")).then_inc(sem, 16)
                if rv:
                    e.dma_start(out=r2_3d(XH_O, KC, 256, 128, 256), in_=x1T[:, :, t0 + T:t0 + T + 128].rearrange("c p t -> p c t")).then_inc(sem, 16)
            P.dma("pool", "xld", lfn, 1 + int(lv) + int(rv), reads=deps, writes=XCALL + ["xh"])

        def proj_center(col0, narrow=False):
            slot, sn = wload(w_in[l, :, col0:col0 + 128], KC)
            bk, bn = bank()
            a, b = (cl, ch) if narrow else (0, T)
            mm_group(bk[:, a:b], bn, [(slot[:, k, :], XC[:, k, a:b]) for k in range(KC)], [sn] + XCALL)
            return slot, sn, bk, bn

        def proj_halo(slot, sn, a=0, b=256):
            bk, bn = bank()
            mm_group(bk[:, a:b], bn, [(slot[:, k, :], XH(k, a, b)) for k in range(KC)], [sn, "xh"])
            return bk, bn

        def rope(dst_ap, wname, src_ap, bn, c0, n):
            raw, rn = tmp()
            P.op("act", lambda e: e.activation(out=raw[:, 0:n], in_=src_ap, func=AF.Copy), reads=[bn], writes=[rn])
            t1, t1n = tmp()
            P.op("dve", lambda e: e.tensor_tensor(out=t1[:, 0:n], in0=raw[:, 0:n], in1=COS[:, c0:c0 + n], op=ALU.mult),
                 reads=[rn, "cos"], writes=[t1n])
            t2, t2n = tmp()

            def f2(e):
                e.tensor_tensor(out=t2[0:64, 0:n], in0=raw[64:128, 0:n], in1=SIN[64:128, c0:c0 + n], op=ALU.mult)
                return e.tensor_tensor(out=t2[64:128, 0:n], in0=raw[0:64, 0:n], in1=SIN[0:64, c0:c0 + n], op=ALU.mult)
            P.op("dve", f2, reads=[rn, "sin"], writes=[t2n])
            P.op("dve", lambda e: e.tensor_tensor(out=dst_ap, in0=t1[:, 0:n], in1=t2[:, 0:n], op=ALU.add),
                 reads=[t1n, t2n], writes=[wname])

        for h in range(8):
            slot, sn, bk, bn = proj_center(h * 128, narrow=True)
            rope(qT(h), "q", bk[:, :], bn, 128, T)
        if l == 0:
            for wb in wbs:
                if not (1 <= wb <= 4):
                    load_block(wb)
        for kv in range(2):
            slot, sn, bk, bn = proj_center(K_OFF + kv * 128)
            bh, bhn = proj_halo(slot, sn)
            rope(kT(kv, 128, 640), "k", bk[:, :], bn, 128, T)
            if lv:
                rope(kT(kv, 0, 128), "k", bh[:, 0:128], bhn, 0, 128)
            if rv:
                rope(kT(kv, 640, 768), "k", bh[:, 128:256], bhn, 640, 128)
        for kv in range(2):
            slot, sn, bk, bn = proj_center(V_OFF + kv * 128)
            bh, bhn = proj_halo(slot, sn)
            copy_op("act", vT(kv, 128, 640), bk[:, :], [bn], ["vT"])
            if lv:
                copy_op("act", vT(kv, 0, 128), bh[:, 0:128], [bhn], ["vT"])
            if rv:
                copy_op("act", vT(kv, 640, 768), bh[:, 128:256], [bhn], ["vT"])
        for wb in wbs:
            bk, bn = bank()

            def vfn(e, bk=bk, wb=wb):
                e.transpose(bk[:, 0:128], vT(0, wb * 128, wb * 128 + 128, F32), IDN[:])
                return e.transpose(bk[:, 128:256], vT(1, wb * 128, wb * 128 + 128, F32), IDN[:])
            P.op("pe", vfn, reads=["vT", "idn"], writes=[bn])
            copy_op("dve", r2(V_O + wb * 256, 256), bk[:, 0:256], [bn], ["V"])
        for j in range(8):
            slot, sn, bk, bn = proj_center(U_OFF + j * 128)
            bh, bhn = proj_halo(slot, sn, 120, 136)
            copy_op("act", uT(j, 8, 520), bk[:, :], [bn], ["u"])
            copy_op("act", uT(j, 0, 8), bh[:, 120:128], [bhn], ["u"])
            copy_op("act", uT(j, 520, 528), bh[:, 128:136], [bhn], ["u"])
        def pool_chunk(j):
            gi = j // 2
            w = POOLW[gi]
            hh = w // 2
            cur = None
            curn = "u"
            m = 1
            ln = 528
            while m < w:
                nxt, nn = tmp()
                ln2 = ln - m

                def pf(e, cur=cur, nxt=nxt, m=m, ln2=ln2, j=j):
                    a = uT(j, 0, ln2, F32) if cur is None else cur[:, 0:ln2]
                    b = uT(j, m, m + ln2, F32) if cur is None else cur[:, m:m + ln2]
                    return e.tensor_tensor(out=nxt[:, 0:ln2], in0=a, in1=b, op=ALU.add)
                P.op("dve", pf, reads=[curn], writes=[nn])
                cur, curn, ln, m = nxt, nn, ln2, m * 2
            t3, t3n = tmp()
            P.op("dve", lambda e, cur=cur, t3=t3, gi=gi, hh=hh: e.tensor_tensor(out=t3[:, 0:T], in0=cur[:, 8 - hh:8 - hh + T], in1=RC[:, gi, :], op=ALU.mult),
                 reads=[curn, "rc"], writes=[t3n])
            P.op("dve", lambda e, t3=t3, j=j: e.tensor_tensor(out=uT(j, 8, 520), in0=t3[:, 0:T], in1=uT(j, 8, 520, F32), op=ALU.subtract),
                 reads=[t3n, "u"], writes=["u"])
        def pool_mix():
            i = st["wslot"] % NWS
            st["wslot"] += 1
            wm = WS[i]

            def pmfn(e, sem, wm=wm):
                e.dma_start(out=wm[:, :, :].rearrange("p (a b) c -> p a (b c)", b=2), in_=w_pm[l].rearrange("g (k p) c -> p (g k) c", p=128)).then_inc(sem, 16)
            P.dma("pool", f"w{i}", pmfn, 1, writes=[f"ws{i}"])
            wmv = wm[:, :, :].rearrange("p (a b) c -> p a (b c)", b=2)
            for g4 in range(4):
                for mo in range(2):
                    jo = g4 * 2 + mo
                    bk, bn = bank()
                    mm_group(bk[:, cl:ch], bn, [(wmv[:, g4 * 2 + ki, mo * 128:(mo + 1) * 128], uT(g4 * 2 + ki, 8 + cl, 8 + ch)) for ki in range(2)],
                             [f"ws{i}", "u"])
                    P.op("act", lambda e, bk=bk, jo=jo: e.activation(out=mxT(jo), in_=bk[:, :], func=AF.Identity, scale=PRM2[:, l * 8 + jo:l * 8 + jo + 1]),
                         reads=[bn, "prm2"], writes=["mx"])

        def kbias(wb):
            if not sl["ctype"]:
                return 0.0
            b = s - 1 + wb
            if b <= 1:
                return PRM2[:, 32:33]
            if b >= nblk - 2:
                return PRM2[:, 33:34]
            return 0.0

        steps = []
        for kv in range(2):
            for qi in range(4):
                wq = qi + 1
                kbs = [kb for kb in (wq - 1, wq, wq + 1) if kb in wbs]
                for kb in kbs:
                    steps.append((kv, qi, kb, kb == kbs[0], kb == kbs[-1]))
        accs = [((BANK[4], "bank4"), (BANK[5], "bank5")), ((BANK[6], "bank6"), (BANK[7], "bank7"))]
        pend = []

        def emit_st(step):
            kv, qi, kb, first, last = step
            wq = qi + 1
            bk, bn = bank((0, 1, 2, 3))
            q4 = R2[:, Q_O + kv * 4 * T: Q_O + (kv * 4 + 4) * T].rearrange("p (h t) -> p h t", t=T)[:, :, qi * 128:(qi + 1) * 128]

            def sfn(e, bk=bk, kb=kb, kv=kv, q4=q4):
                return e.matmul(bk[:, :].rearrange("p (h t) -> p h t", t=128), lhsT=kT(kv, kb * 128, kb * 128 + 128), rhs=q4,
                                start=True, stop=True)
            P.op("pe", sfn, reads=["k", "q", "cstr"], writes=[bn])
            pt, ptn = ptile()
            kb_b = kbias(kb)
            rd = [bn] + (["prm2"] if not isinstance(kb_b, float) else [])
            P.op("act", lambda e, pt=pt, bk=bk, kb_b=kb_b: e.activation(out=pt[:], in_=bk[:, :], func=AF.Exp, bias=kb_b, scale=SCALE),
                 reads=rd, writes=[ptn])
            if kb != wq:
                P.op("pool", lambda e, pt=pt, which=(0 if kb < wq else 1): e.tensor_tensor(out=pt[:], in0=pt[:], in1=MB(which), op=ALU.mult),
                     reads=[ptn, "cstr"], writes=[ptn])
            return pt, ptn

        def emit_pv(step, pt, ptn):
            kv, qi, kb, first, last = step
            (ob, obn), (db, dbn) = accs[(kv * 4 + qi) % 2]
            mm_group(ob[:, :], obn, [(Vtok(kb, kv), pt[:])], ["V", ptn], start=first, stop=last)
            mm_group(db[:, :], dbn, [(ONESR, pt[:])], ["cstr", ptn], start=first, stop=last)
            if last:
                rt, rtn = tmp()
                def addsink(e):
                    ins = None
                    for h in range(4):
                        ins = e.tensor_scalar(out=rt[:, h * 128:(h + 1) * 128], in0=db[:, h * 128:(h + 1) * 128],
                                              scalar1=ESINK[:, kv * 4 + h:kv * 4 + h + 1], scalar2=None, op0=ALU.add)
                    return ins
                P.op("dve", addsink, reads=[dbn, "esink"], writes=[rtn])
                P.op("dve", lambda e: e.reciprocal(out=rt[:, 0:T], in_=rt[:, 0:T]), reads=[rtn], writes=[rtn])
                dst = R2[:, XH_O + kv * 4 * T: XH_O + (kv * 4 + 4) * T].rearrange("p (h t) -> p h t", t=T)[:, :, qi * 128:(qi + 1) * 128]
                P.op("dve", lambda e: e.tensor_tensor(out=dst, in0=ob[:, :].rearrange("p (h t) -> p h t", t=128),
                                                      in1=rt[:, 0:T].rearrange("p (h t) -> p h t", t=128), op=ALU.mult),
                     reads=[obn, rtn], writes=["o"])

        SKEW = 3
        for idx, step in enumerate(steps):
            pt, ptn = emit_st(step)
            pend.append((step, pt, ptn))
            if len(pend) > SKEW:
                emit_pv(*pend.pop(0))
        while pend:
            emit_pv(*pend.pop(0))

        for m in range(16):
            slot, sn = wload(w_in[l, :, G_OFF + m * 128: G_OFF + (m + 1) * 128], KC)
            bg, bgn = bank()
            mm_group(bg[:, cl:ch], bgn, [(slot[:, k, :], XC[:, k, cl:ch]) for k in range(KC)], [sn] + XCALL)
            slot, sn = wload(w_ap[l, :, m * 128:(m + 1) * 128], 8)
            ba, ban = bank()
            mm_group(ba[:, cl:ch], ban, [(slot[:, k, :], oT(k, cl, ch)) for k in range(8)], [sn, "o"])
            t1, t1n = tmp()
            P.op("act", lambda e, t1=t1, bg=bg: e.activation(out=t1[:, 0:T], in_=bg[:, :], func=AF.Sigmoid), reads=[bgn], writes=[t1n])
            P.op("dve", lambda e, t1=t1, ba=ba, m=m: e.tensor_tensor(out=mixT(m), in0=t1[:, 0:T], in1=ba[:, :], op=ALU.mult),
                 reads=[t1n, ban], writes=[f"mix{m}"])
            if m % 2 == 1:
                pool_chunk(m // 2)
        pool_mix()
        for m in range(16):
            slot, sn = wload(w_pp[l, :, m * 128:(m + 1) * 128], 8)
            bb, bbn = bank()
            mm_group(bb[:, cl:ch], bbn, [(slot[:, k, :], mxT(k, cl, ch)) for k in range(8)], [sn, "mx"])
            slot, sn = wload(w_in[l, :, G_OFF + D + m * 128: G_OFF + D + (m + 1) * 128], KC)
            bg2, bg2n = bank()
            mm_group(bg2[:, cl:ch], bg2n, [(slot[:, k, :], XC[:, k, cl:ch]) for k in range(KC)], [sn] + XCALL)
            t2, t2n = tmp()
            P.op("act", lambda e, t2=t2, bg2=bg2: e.activation(out=t2[:, 0:T], in_=bg2[:, :], func=AF.Sigmoid), reads=[bg2n], writes=[t2n])
            P.op("dve", lambda e, t2=t2, bb=bb: e.tensor_tensor(out=t2[:, 0:T], in0=t2[:, 0:T], in1=bb[:, :], op=ALU.mult),
                 reads=[t2n, bbn], writes=[t2n])
            P.op("dve", lambda e, t2=t2, m=m: e.tensor_tensor(out=mixT(m), in0=mixT(m, F32), in1=t2[:, 0:T], op=ALU.add),
                 reads=[t2n, f"mix{m}"], writes=[f"mix{m}"])

        def res_ln(wsrc_fn, nk_list, rhs_fn, rhs_name, goff, boff, final=False):
            sb, sbn = BANK[6], "bank6"
            qb_, qbn = BANK[7], "bank7"
            pend_st = []

            def stats_mm(m, sq, sqn):
                mm_group(sb[:, cl:ch], sbn, [(ONESR, XC[:, m, cl:ch])], ["cstr", XCALL[m]], start=(m == 0), stop=(m == 15))
                mm_group(qb_[:, cl:ch], qbn, [(ONESR, sq[:, cl:ch])], ["cstr", sqn], start=(m == 0), stop=(m == 15))

            for m in range(16):
                bk, bn = bank()
                k0 = 0
                nparts = len(nk_list)
                for pi, nk in enumerate(nk_list):
                    slot, sn = wload(wsrc_fn(m, k0, nk), nk)
                    sub = 4 if m == 0 else nk
                    for a0 in range(0, nk, sub):
                        a1 = min(a0 + sub, nk)
                        mm_group(bk[:, cl:ch], bn, [(slot[:, k, :], rhs_fn(k0 + k)) for k in range(a0, a1)],
                                 [sn] + [f"{rhs_name}{k0 + k}" for k in range(a0, a1)],
                                 start=(pi == 0 and a0 == 0), stop=(pi == nparts - 1 and a1 == nk))
                    k0 += nk
                P.op("dve", lambda e, bk=bk, m=m: e.scalar_tensor_tensor(out=XC[:, m, :], in0=XC[:, m, :], scalar=ALPHA, in1=bk[:, :],
                                                                          op0=ALU.mult, op1=ALU.add),
                     reads=[bn, XCALL[m]], writes=[XCALL[m]])
                sq, sqn = SQ[m % 2], f"sq{m % 2}"
                P.op("act", lambda e, sq=sq, m=m: e.activation(out=sq[:], in_=XC[:, m, :], func=AF.Square),
                     reads=[XCALL[m]], writes=[sqn])
                pend_st.append((m, sq, sqn))
                if len(pend_st) > 1:
                    stats_mm(*pend_st.pop(0))
            while pend_st:
                stats_mm(*pend_st.pop(0))
            mean, rstd = LNT
            negmean = mean[:].bitcast(F32R)
            P.op("act", lambda e: e.activation(out=negmean, in_=sb[:, :], func=AF.Copy, scale=-1.0 / D), reads=[sbn], writes=["lnt0"])
            t1, t1n = tmp()
            P.op("act", lambda e: e.activation(out=t1[:, 0:T], in_=sb[:, :], func=AF.Square, scale=1.0 / D), reads=[sbn], writes=[t1n])
            P.op("dve", lambda e: e.scalar_tensor_tensor(out=rstd[:], in0=qb_[:, :], scalar=1.0 / D, in1=t1[:, 0:T], op0=ALU.mult, op1=ALU.subtract),
                 reads=[qbn, t1n], writes=["lnt1"])
            P.op("act", lambda e: e.activation(out=rstd[:], in_=rstd[:], func=AF.Sqrt, bias=EPSC), reads=["lnt1", "prm2"], writes=["lnt1"])
            P.op("dve", lambda e: e.reciprocal(out=rstd[:], in_=rstd[:]), reads=["lnt1"], writes=["lnt1"])
            for m in range(16):
                t2, t2n = tmp()
                cb, cbn = bank((0, 1, 2, 3, 4, 5))
                mm_group(cb[:, cl:ch], cbn, [(IDR, XC[:, m, cl:ch]), (IDR, negmean[:, cl:ch])], ["cstr", XCALL[m], "lnt0"])
                P.op("dve", lambda e, t2=t2, cb=cb: e.tensor_tensor(out=t2[:, 0:T], in0=cb[:, :], in1=rstd[:], op=ALU.mult),
                     reads=[cbn, "lnt1"], writes=[t2n])
                P.op("act", lambda e, t2=t2, m=m: e.activation(out=XC[:, m, :], in_=t2[:, 0:T], func=AF.Identity,
                                                              scale=PRM[:, goff + l * 16 + m: goff + l * 16 + m + 1],
                                                              bias=PRM[:, boff + l * 16 + m: boff + l * 16 + m + 1]),
                     reads=[t2n, "prm"], writes=[XCALL[m]])

        if next_unit is not None:
            prefetch(*next_unit)
        res_ln(lambda m, k0, nk: w_out[l, k0 * 128:(k0 + nk) * 128, m * 128:(m + 1) * 128], [16], lambda k: mixT(k, F32R, cl, ch), "mix", 0, 32)

        for j in range(NJ):
            slot, sn = wload(w_f1[l, :, j * 128:(j + 1) * 128], KC)
            b1, b1n = bank()
            mm_group(b1[:, cl:ch], b1n, [(slot[:, k, :], XC[:, k, cl:ch]) for k in range(KC)], [sn] + XCALL)
            slot, sn = wload(w_f1[l, :, DFF + j * 128: DFF + (j + 1) * 128], KC)
            b2, b2n = bank()
            mm_group(b2[:, cl:ch], b2n, [(slot[:, k, :], XC[:, k, cl:ch]) for k in range(KC)], [sn] + XCALL)
            P.op("act", lambda e, b1=b1, j=j: e.activation(out=gT(j), in_=b1[:, :], func=AF.Silu), reads=[b1n], writes=[f"g{j}"])
            P.op("dve", lambda e, b2=b2, j=j: e.tensor_tensor(out=gT(j), in0=gT(j, F32), in1=b2[:, :], op=ALU.mult),
                 reads=[b2n, f"g{j}"], writes=[f"g{j}"])
        res_ln(lambda m, k0, nk: w_f2[l, k0 * 128:(k0 + nk) * 128, m * 128:(m + 1) * 128], [16, 16, 12], lambda k: gT(k, F32R, cl, ch), "g", 64, 96, final=(l == DEPTH - 1))

        if l == 0:
            if sl["ctype"] and s == 0:
                P.op("dve", lambda e: e.tensor_scalar(out=XC[:, :, 0:256], in0=XC[:, :, 0:256], scalar1=PRM2[:, 34:35], scalar2=None, op0=ALU.mult),
                     reads=XCALL + ["prm2"], writes=XCALL)
            if sl["ctype"] and s + 4 >= nblk:
                P.op("dve", lambda e: e.tensor_scalar(out=XC[:, :, 256:512], in0=XC[:, :, 256:512], scalar1=PRM2[:, 35:36], scalar2=None, op0=ALU.mult),
                     reads=XCALL + ["prm2"], writes=XCALL)
            t0 = sl["in_off"] + s * 128
            for c0 in range(0, KC, 4):
                def sfn(e, sem, t0=t0, c0=c0):
                    e.dma_start(out=x1T[c0:c0 + 4, :, t0:t0 + T].rearrange("c p t -> p c t"), in_=XC[:, c0:c0 + 4, :].bitcast(F32)).then_inc(sem, 16)
                P.dma("sp", f"x1st{c0}", sfn, 1, reads=XCALL[c0:c0 + 4], writes=[f"x1:{name}:{s // 4}:{c0}"])
        else:
            for qi in range(4):
                STG, sgn = stage()
                for c0 in range(0, KC, 4):
                    bk, bn = bank()

                    def tfn3(e, bk=bk, c0=c0, qi=qi):
                        ins = None
                        for c in range(4):
                            ins = e.transpose(bk[:, c * 128:(c + 1) * 128], XC[:, c0 + c, qi * 128:(qi + 1) * 128].bitcast(F32), IDN[:])
                        return ins
                    P.op("pe", tfn3, reads=XCALL[c0:c0 + 4] + ["idn"], writes=[bn])
                    copy_op(evac_eng(), STG[:, c0 * 128:(c0 + 4) * 128], bk[:, :], [bn], [sgn])
                row0 = sl["out_off"] + (s - sl["out_b0"] + qi) * 128

                def ofn(e, sem, row0=row0, STG=STG):
                    e.dma_start(out=yout[row0:row0 + 128, :], in_=STG[:]).then_inc(sem, 16)
                P.dma("sp", "yst" + sgn, ofn, 1, reads=[sgn], writes=["yout" + sgn])

    units = [(l, sl, s) for l in range(DEPTH) for sl in slots for s in (sl["l1"] if l == 0 else sl["l2"])]
    for i, (l, sl, s) in enumerate(units):
        if i == 0 or units[i - 1][0] != l:
            layer_setup(l)
        group(l, sl, s, units[i + 1] if i + 1 < len(units) else None)
    P.finalize()

    names = P.sem_names()
    with ExitStack() as es:
        sems = {n: es.enter_context(nc.semaphore(n)) for n in names}
        block = es.enter_context(nc.Block())

        @block.tensor
        def _(e):
            P.emit("pe", e, sems)

        @block.scalar
        def _(e):
            P.emit("act", e, sems)

        @block.vector
        def _(e):
            P.emit("dve", e, sems)

        @block.gpsimd
        def _(e):
            P.emit("pool", e, sems)

        @block.sync
        def _(e):
            P.emit("sp", e, sems)
            for k_ in ("yststg0", "yststg1"):
                if k_ in P.dma_cnt:
                    e.wait_ge(sems[k_], P.dma_cnt[k_])
    return nc


def _tables(pos_list):
    pos = np.asarray(pos_list, dtype=np.float32)
    half = 64
    inv = (10000.0 ** (-np.arange(half, dtype=np.float32) / half)).astype(np.float32)
    ang = pos[None, :] * inv[:, None]
    c = np.cos(ang).astype(np.float32)
    s_ = np.sin(ang).astype(np.float32)
    return np.concatenate([c, c], 0), np.concatenate([s_, -s_], 0)


def _rc(pos, S):
    out = np.zeros((4, len(pos)), np.float32)
    pos = np.asarray(pos)
    for gi, w in enumerate(POOLW):
        lo = np.clip(pos - w // 2, 0, S)
        hi = np.clip(pos + (w - w // 2), 0, S)
        cnt = (hi - lo).astype(np.float32)
        out[gi] = np.where(cnt > 0, 1.0 / np.maximum(cnt, 1.0), 0.0)
    return out


def _consts():
    cst = np.zeros((128, 1280), np.float32)
    cst[:, 0:128] = np.eye(128, dtype=np.float32)
    cst[:, 128:256] = 1.0
    j = np.arange(128)[:, None]
    i = np.arange(128)[None, :]
    prev = np.where(j >= i, 1.0, 0.0).astype(np.float32)
    nxt = np.where(j <= i, 1.0, 0.0).astype(np.float32)
    cst[:, 256:768] = np.tile(prev, (1, 4))
    cst[:, 768:1280] = np.tile(nxt, (1, 4))
    return cst


def _fm(v):
    L = v.shape[0]
    n = v.shape[1] // 128
    return np.ascontiguousarray(v.reshape(L, n, 128).transpose(2, 0, 1).reshape(128, L * n))


_NC_CACHE = {}


def kernel(x_prompt, x_sample, w_in, sink, w_attn_proj, w_pool_mix, pool_scale, w_pool_proj, w_out,
           ln1_g, ln1_b, w_ffn_in, w_ffn_out, ln2_g, ln2_b):
    f = lambda a: np.ascontiguousarray(np.asarray(a, dtype=np.float32))
    x_prompt, x_sample = f(x_prompt), f(x_sample)
    ncores = 8
    cst = _consts()
    prm = np.concatenate([_fm(f(ln1_g)), _fm(f(ln1_b)), _fm(f(ln2_g)), _fm(f(ln2_b))], axis=1)
    shared = dict(w_in=f(w_in), w_attn_proj=f(w_attn_proj), w_pool_mix=f(w_pool_mix), w_pool_proj=f(w_pool_proj),
                  w_out=f(w_out), w_ffn_in=f(w_ffn_in), w_ffn_out=f(w_ffn_out), prm=np.ascontiguousarray(prm), cst=cst)
    sinkb = np.broadcast_to(f(sink).reshape(1, 16), (128, 16))
    in_maps = []
    for c in range(ncores):
        sq, half = divmod(c, 2)
        xin = np.zeros((6656, D), np.float32)
        xin[0:2048] = x_prompt[2 * c]
        xin[2048:4096] = x_prompt[2 * c + 1]
        w0 = half * 2048 - 256
        lo, hi = max(w0, 0), min(w0 + 2560, 4096)
        xin[4096 + (lo - w0): 4096 + (hi - w0)] = x_sample[sq, lo:hi]
        posC = np.arange(w0, w0 + 2560)
        posAB = np.arange(2048)
        cAB, sAB = _tables(posAB)
        cC, sC = _tables(posC)
        cosT = np.concatenate([cAB, cC], 1)
        sinT = np.concatenate([sAB, sC], 1)
        rc = np.concatenate([_rc(posAB, 2048), _rc(posC, 4096)], 1)
        rcT = np.ascontiguousarray(np.broadcast_to(rc[None], (128, 4, 4608)))
        prm2 = np.zeros((128, 40), np.float32)
        prm2[:, 0:16] = _fm(f(pool_scale))
        prm2[:, 16:32] = sinkb
        lvalid, rvalid = (half == 1), (half == 0)
        prm2[:, 32] = 0.0 if lvalid else NEG
        prm2[:, 33] = 0.0 if rvalid else NEG
        prm2[:, 34] = 1.0 if lvalid else 0.0
        prm2[:, 35] = 1.0 if rvalid else 0.0
        prm2[:, 36] = EPS
        m = dict(shared)
        m.update(xin=xin, cosT=np.ascontiguousarray(cosT), sinT=np.ascontiguousarray(sinT), rcT=rcT, prm2=prm2)
        in_maps.append(m)
    if "full" not in _NC_CACHE:
        _NC_CACHE["full"] = build_program(FULL_CFG)
    nc = _NC_CACHE["full"]
    res = run_bass_kernel_spmd(nc, in_maps, core_ids=list(range(ncores)))
    y_prompt = np.empty((16, 2048, D), np.float32)
    y_sample = np.empty((4, 4096, D), np.float32)
    for c in range(ncores):
        y = np.asarray(res.results[c]["yout"])
        sq, half = divmod(c, 2)
        y_prompt[2 * c] = y[0:2048]
        y_prompt[2 * c + 1] = y[2048:4096]
        y_sample[sq, half * 2048:(half + 1) * 2048] = y[4096:6144]
    return (y_prompt, y_sample)
```
